# Optimizing a Trainium2 kernel written in Bass

```python
import jax, jax.numpy as jnp
from jax import lax
import numpy as np

D_MODEL = 2048
BATCH = 4
SEQ = 4096
DEPTH = 2

CHUNK = 64

D_A = 768
K_A = 31
POOL_WINDOWS = (2, 4, 8, 16)
N_POOL_GROUPS = 4
POOL_GROUP = 128
D_B = N_POOL_GROUPS * POOL_GROUP
D_C = 768
K_C = 3
D_IN = 2 * D_A + D_B + 3 * D_C
N_BRANCH = 3
D_FF = 4 * D_MODEL
EPS = 1e-6

kernel_name = "hybrid_conformer_pool_shortconv_gated"


def rmsnorm(x, g):
    xf = x.astype(jnp.float32)
    y = xf * lax.rsqrt(jnp.mean(xf * xf, axis=-1, keepdims=True) + EPS)
    return (y * g.astype(jnp.float32)).astype(x.dtype)


def layernorm(x, g, b):
    xf = x.astype(jnp.float32)
    mu = jnp.mean(xf, axis=-1, keepdims=True)
    xc = xf - mu
    var = jnp.mean(xc * xc, axis=-1, keepdims=True)
    y = xc * lax.rsqrt(var + EPS) * g.astype(jnp.float32) + b.astype(jnp.float32)
    return y.astype(x.dtype)


def causal_dwconv(u, w):
    k, c = w.shape
    return lax.conv_general_dilated(
        u, w[:, None, :].astype(u.dtype), window_strides=(1,), padding=[(k - 1, 0)],
        dimension_numbers=("NWC", "WIO", "NWC"), feature_group_count=c)


def multiscale_pool(u, w_grp, scale):
    b, s, _ = u.shape
    ug = u.reshape(b, s, N_POOL_GROUPS, POOL_GROUP).astype(jnp.float32)
    cs = jnp.cumsum(ug, axis=1)
    pos = jnp.arange(1, s + 1, dtype=jnp.float32)
    means = []
    for gi, w in enumerate(POOL_WINDOWS):
        c = cs[:, :, gi]
        lag = jnp.pad(c, ((0, 0), (w, 0), (0, 0)))[:, :s]
        cnt = jnp.minimum(pos, float(w))[None, :, None]
        means.append((c - lag) / cnt)
    pooled = (jnp.stack(means, axis=2) - ug).astype(u.dtype)
    mixed = jnp.einsum("bsgc,gcd->bsgd", pooled, w_grp)
    return mixed.reshape(b, s, D_B) * scale


def setup_inputs(seed: int = 0) -> dict:
    key = jax.random.key(seed)
    ks = jax.random.split(key, 24)
    f32 = jnp.float32
    nrm = lambda k, shape, scale: jax.random.normal(k, shape, f32) * scale
    L, D = DEPTH, D_MODEL
    return {
        "x": jax.random.normal(ks[0], (BATCH, SEQ, D), f32),
        "g_mix": 1.0 + nrm(ks[1], (L, D), 0.02),
        "w_in": nrm(ks[2], (L, D, D_IN), D ** -0.5),
        "w_gate": nrm(ks[3], (L, D, N_BRANCH * D), D ** -0.5),
        "b_gate": nrm(ks[4], (L, N_BRANCH * D), 0.01),
        "b_glu": nrm(ks[5], (L, 2 * D_A), 0.01),
        "w_dw_a": nrm(ks[6], (L, K_A, D_A), K_A ** -0.5),
        "b_dw_a": nrm(ks[7], (L, D_A), 0.01),
        "ln_g_a": 1.0 + nrm(ks[8], (L, D_A), 0.02),
        "ln_b_a": nrm(ks[9], (L, D_A), 0.01),
        "w_a_out": nrm(ks[10], (L, D_A, D), D_A ** -0.5),
        "w_pool_grp": nrm(ks[11], (L, N_POOL_GROUPS, POOL_GROUP, POOL_GROUP), POOL_GROUP ** -0.5),
        "pool_scale": 1.0 + nrm(ks[12], (L, D_B), 0.02),
        "w_b_out": nrm(ks[13], (L, D_B, D), D_B ** -0.5),
        "w_dw_c": nrm(ks[14], (L, K_C, D_C), K_C ** -0.5),
        "w_c_out": nrm(ks[15], (L, D_C, D), D_C ** -0.5),
        "w_o": nrm(ks[16], (L, D, D), D ** -0.5),
        "g_mlp": 1.0 + nrm(ks[17], (L, D), 0.02),
        "w_up": nrm(ks[18], (L, D, D_FF), D ** -0.5),
        "w_down": nrm(ks[19], (L, D_FF, D), 0.5 * D_FF ** -0.5),
        "g_final": 1.0 + nrm(ks[20], (D,), 0.02),
    }


def reference(x, g_mix, w_in, w_gate, b_gate, b_glu, w_dw_a, b_dw_a, ln_g_a, ln_b_a, w_a_out,
              w_pool_grp, pool_scale, w_b_out, w_dw_c, w_c_out, w_o, g_mlp, w_up, w_down, g_final):
    b, s, d = x.shape
    for l in range(DEPTH):
        h = rmsnorm(x, g_mix[l])
        z = h @ w_in[l]
        za = z[..., : 2 * D_A] + b_glu[l]
        zb = z[..., 2 * D_A: 2 * D_A + D_B]
        zc = z[..., 2 * D_A + D_B:]

        ua = za[..., :D_A] * jax.nn.sigmoid(za[..., D_A:])
        va = causal_dwconv(ua, w_dw_a[l]) + b_dw_a[l]
        va = jax.nn.silu(layernorm(va, ln_g_a[l], ln_b_a[l]))
        ya = va @ w_a_out[l]

        yb = multiscale_pool(zb, w_pool_grp[l], pool_scale[l]) @ w_b_out[l]

        gb, gc, xv = zc[..., :D_C], zc[..., D_C: 2 * D_C], zc[..., 2 * D_C:]
        yc = (gb * causal_dwconv(gc * xv, w_dw_c[l])) @ w_c_out[l]

        gates = jax.nn.sigmoid(h @ w_gate[l] + b_gate[l]).reshape(b, s, N_BRANCH, d)
        merged = gates[:, :, 0] * ya + gates[:, :, 1] * yb + gates[:, :, 2] * yc
        x = x + merged @ w_o[l]

        h2 = rmsnorm(x, g_mlp[l])
        x = x + jnp.square(jax.nn.relu(h2 @ w_up[l])) @ w_down[l]
    return rmsnorm(x, g_final)
```

```python
import numpy as np
import concourse.bass as bass
import concourse.mybir as mybir
from concourse.bass_utils import run_bass_kernel_spmd

F32 = mybir.dt.float32
BF16 = mybir.dt.bfloat16
AF = mybir.ActivationFunctionType
ALU = mybir.AluOpType
EPS = 1e-6
USE_LNEXP = True
POOL_W = (2, 4, 8, 16)


class Cfg:
    def __init__(self, **kw):
        self.D = 2048
        self.DA = 768
        self.DC = 768
        self.DFF = 8192
        self.KTAP = 31
        self.depth = 2
        self.batch = 4
        self.seq = 4096
        self.ncores = 8
        self.T = 424
        self.NT = 5
        self.HALO = 72
        self.G = 16
        self.NSLOT = 8
        self.KP = 16
        for k, v in kw.items():
            setattr(self, k, v)
        self.KD = self.D // 128
        self.KA = self.DA // 128
        self.KB = 4
        self.KC = self.DC // 128
        self.KF = self.DFF // 128
        self.DB = 512
        self.DIN = 2 * self.DA + self.DB + 3 * self.DC
        self.TOK = self.batch * self.seq // self.ncores
        assert self.NT * self.T - self.HALO == self.TOK
        self.cps = self.ncores // self.batch
        self.HA = self.KTAP - 1
        self.HB = 15
        self.HC = 2


def weight_segments(c):
    KD, KA, KB, KC, KF = c.KD, c.KA, c.KB, c.KC, c.KF
    segs = []
    nil = min(2, KA)
    segs.append(("il", [("w_in", cb) for j in range(nil) for cb in (j, KA + j)], KD))
    for j in range(KA):
        if j >= nil:
            segs.append(("w_in", j, KD))
            segs.append(("w_in", KA + j, KD))
        if j >= 1:
            segs.append(("dwa", j - 1, c.KP))
    for g in range(KB):
        segs.append(("w_in", 2 * KA + g, KD))
        if g == 0:
            segs.append(("dwa", KA - 1, c.KP))
    for j in range(KC):
        segs.append(("w_in", 2 * KA + KB + KC + j, KD))
        segs.append(("w_in", 2 * KA + KB + 2 * KC + j, KD))
        segs.append(("w_in", 2 * KA + KB + j, KD))
    for g in range(KB):
        segs.append(("w_pool", g, 1))
    for m in range(KD):
        segs.append(("w_gate", m, KD))
        segs.append(("w_gate", KD + m, KD))
        segs.append(("w_gate", 2 * KD + m, KD))
        segs.append(("w_a_out", m, KA))
        segs.append(("w_b_out", m, KB))
        segs.append(("w_c_out", m, KC))
    for n in range(KD):
        segs.append(("w_o", n, KD))
    segs.append(("il", [("w_up", j) for j in range(min(4, KF))], KD))
    for j in range(min(4, KF), KF):
        segs.append(("w_up", j, KD))
    for n in range(KD):
        segs.append(("w_down", n, KF))
    return segs


class VecLayout:
    def __init__(self, c):
        self.off = {}
        self.n = 0
        for l in range(c.depth):
            self.reg(("g_mix", l), c.KD)
            self.reg(("b_val", l), c.KA)
            self.reg(("b_gt", l), c.KA)
            self.reg(("w_dw_a", l), c.KTAP * c.KA)
            self.reg(("b_dw_a", l), c.KA)
            self.reg(("ln_g", l), c.KA)
            self.reg(("ln_b", l), c.KA)
            self.reg(("pool_scale", l), c.KB)
            self.reg(("w_dw_c", l), 3 * c.KC)
            self.reg(("b_gate", l), 3 * c.KD)
            self.reg(("g_mlp", l), c.KD)
        self.reg("g_final", c.KD)
        self.reg("mask", 1)
        self.reg("eps", 1)
        self.reg("rc0", 4 * 16)
        self.reg("ident", 128)

    def reg(self, name, n):
        self.off[name] = self.n
        self.n += n


class _Sem:
    def __init__(self, h):
        self.h = h
        self.count = 0


class _Op:
    __slots__ = ("eng", "fn", "deps", "token", "is_dma", "dsem", "signal", "idx")


class Prog:
    ENGS = ("pe", "act", "dve", "pool", "sp")

    def __init__(self):
        self.ops = {e: [] for e in self.ENGS}
        self.last_writer = {}
        self.readers = {}
        self.nops = 0

    def add(self, eng, fn, reads=(), writes=(), dsem=None):
        op = _Op()
        op.eng = eng
        op.fn = fn
        op.is_dma = dsem is not None
        op.dsem = dsem
        op.signal = False
        op.token = None
        op.idx = self.nops
        self.nops += 1
        deps = {}
        for r in reads:
            w = self.last_writer.get(r)
            if w is not None:
                deps[w.idx] = w
        for wr in writes:
            w = self.last_writer.get(wr)
            if w is not None:
                deps[w.idx] = w
            for rd in self.readers.get(wr, ()):
                deps[rd.idx] = rd
        deps.pop(op.idx, None)
        op.deps = list(deps.values())
        for r in reads:
            self.readers.setdefault(r, []).append(op)
        for wr in writes:
            self.last_writer[wr] = op
            self.readers[wr] = []
        if op.is_dma:
            dsem.count += 16
            op.token = (dsem, dsem.count)
        self.ops[eng].append(op)
        return op

    def finalize(self, esems):
        for e in self.ENGS:
            for op in self.ops[e]:
                for d in op.deps:
                    if d.is_dma:
                        continue
                    if d.eng == "pe" and op.eng == "pe" and not op.is_dma:
                        continue
                    d.signal = True
        for e in self.ENGS:
            s = esems[e]
            for op in self.ops[e]:
                if not op.is_dma and op.signal:
                    s.count += 1
                    op.token = (s, s.count)

    def emit(self, eng_name, eng, final_waits=()):
        waited = {}
        for op in self.ops[eng_name]:
            for d in op.deps:
                if (not d.is_dma) and d.eng == "pe" and eng_name == "pe" and not op.is_dma:
                    continue
                sem, val = d.token
                if waited.get(id(sem), 0) >= val:
                    continue
                waited[id(sem)] = val
                eng.wait_ge(sem.h, val)
            inst = op.fn(eng)
            if op.is_dma:
                inst.then_inc(op.dsem.h, 16)
            elif op.signal:
                inst.then_inc(op.token[0].h, 1)
        for sem in final_waits:
            if sem.count > 0:
                eng.wait_ge(sem.h, sem.count)


def build_program(c):
    T, NT, HALO = c.T, c.NT, c.HALO
    KD, KA, KB, KC, KF = c.KD, c.KA, c.KB, c.KC, c.KF
    HA, HB, HC = c.HA, c.HB, c.HC
    G, NSLOT = c.G, c.NSLOT
    VL = VecLayout(c)
    segs = weight_segments(c)
    tiles_per_layer = sum((len(s[1]) * s[2]) if s[0] == "il" else s[2] for s in segs)
    ntiles = tiles_per_layer * c.depth
    ngroups = (ntiles + G - 1) // G
    ntp = ngroups * G

    nc = bass.Bass("TRN2", target_bir_lowering=False)
    xin = nc.dram_tensor("xin", [NT * T, c.D], F32, kind="ExternalInput").ap()
    wf = nc.dram_tensor("wf", [128, ntp * 128], F32, kind="ExternalInput").ap()
    vecs_d = nc.dram_tensor("vecs", [128, VL.n], F32, kind="ExternalInput").ap()
    wb = nc.dram_tensor("wb", [128, ntp * 128], BF16, kind="Internal").ap()
    yout = nc.dram_tensor("yout", [c.TOK, c.D], F32, kind="ExternalOutput").ap()

    def words(n_elem, dt):
        return (n_elem * (4 if dt == F32 else 2) + 3) // 4
    lay = {}
    cur = 0
    for name, n_elem, dt in (
        ("ua", KA * (HA + T), BF16),
        ("vaf", KA * T, F32),
        ("zb", KB * (HB + T), F32),
        ("vx", KC * (HC + T), F32),
        ("mo", (KA + KB + KC) * T, BF16),
        ("mg", KD * T, BF16),
        ("pl", KB * T, BF16),
    ):
        lay[name] = (cur, words(n_elem, dt))
        cur += words(n_elem, dt)
    scr_words = max(cur, words(KF * T, BF16), words(KD * T, F32))
    NTMP = 12
    TW = HB + T + 1

    import contextlib
    with contextlib.ExitStack() as es:
        def sb(name, shape, dt):
            return es.enter_context(nc.sbuf_tensor(name, shape, dt))

        xres = sb("xres", [128, KD, T], F32)
        hb = sb("hb", [128, KD, T], BF16)
        scr = sb("scr", [128, scr_words], F32)
        tmp = sb("tmp", [128, NTMP, TW], F32)
        ring = sb("ring", [128, NSLOT, G * 128], BF16)
        stg_in = sb("stg_in", [128, 2, c.D], F32)
        stg_out = sb("stg_out", [128, 2, c.D], F32)
        vecs = sb("vecs_sb", [128, VL.n], F32)
        ones_bf = sb("ones_bf", [128, 128], BF16)
        carA = sb("carA", [128, c.depth * KA, HA], BF16)
        carB = sb("carB", [128, c.depth * KB, HB], F32)
        carC = sb("carC", [128, c.depth * KC, HC], F32)
        ps = [es.enter_context(nc.psum_tensor(f"ps{i}", [128, 512], F32)) for i in range(8)]

        def view(name, dt, k):
            o, w = lay[name]
            v = scr[:, o:o + w]
            if dt == BF16:
                v = v.bitcast(BF16)
            n = (w * (1 if dt == F32 else 2)) // k * k
            return v[:, 0:n].rearrange("p (k t) -> p k t", k=k)

        ua = view("ua", BF16, KA)
        vaf = view("vaf", F32, KA)
        zbx = view("zb", F32, KB)
        vx = view("vx", F32, KC)
        mo = view("mo", BF16, KA + KB + KC)
        mg = view("mg", BF16, KD)
        pl = view("pl", BF16, KB)
        actv = scr[:, 0:words(KF * T, BF16)].bitcast(BF16).rearrange("p (k t) -> p k t", k=KF)
        ofv = scr[:, 0:KD * T].rearrange("p (k t) -> p k t", k=KD)

        sem_names = ["pe", "act", "dve", "pool", "sp"]
        esems = {n: _Sem(es.enter_context(nc.semaphore("s_" + n))) for n in sem_names}
        ld_sems = [_Sem(es.enter_context(nc.semaphore(f"ld{i}"))) for i in range(NSLOT)]
        st_sems = [_Sem(es.enter_context(nc.semaphore(f"st{i}"))) for i in range(NSLOT)]
        xin_sems = [_Sem(es.enter_context(nc.semaphore(f"xi{i}"))) for i in range(2)]
        out_sems = [_Sem(es.enter_context(nc.semaphore(f"xo{i}"))) for i in range(2)]
        misc_sem = _Sem(es.enter_context(nc.semaphore("misc")))

        P = Prog()

        def vcol(name, i=0):
            o = VL.off[name] + i
            return vecs[:, o:o + 1]

        class WStream:
            def __init__(self):
                self.consumed = 0
                self.next_dma = 0
                self.total = ntp * NT

            def slot_of_group(self, gq):
                return gq % NSLOT

            def pump(self):
                while self.next_dma * G < self.total and (self.next_dma - NSLOT + 1) * G <= self.consumed:
                    gq = self.next_dma
                    s = gq % NSLOT
                    q = gq % ngroups
                    p = gq // ngroups
                    wp = min(((0,) + (1,) * 2 + (2,) * 4 + (3,) * 8)[q % 15], max(NT - 2, 0))
                    dst = ring[:, s, :]
                    if p <= wp:
                        src = wf[:, q * G * 128:(q + 1) * G * 128]
                        P.add("pool", (lambda e, dst=dst, src=src: e.dma_start(out=dst, in_=src, max_dma_last_dim=8192)),
                              writes=[("ws", s)], dsem=ld_sems[s])
                        if p == wp and NT > 1:
                            dd = wb[:, q * G * 128:(q + 1) * G * 128]
                            P.add("sp", (lambda e, dst=dst, dd=dd: e.dma_start(out=dd, in_=dst)),
                                  reads=[("ws", s)], writes=[("wb", q)], dsem=st_sems[s])
                    else:
                        src = wb[:, q * G * 128:(q + 1) * G * 128]
                        P.add("sp", (lambda e, dst=dst, src=src: e.dma_start(out=dst, in_=src)),
                              reads=[("wb", q)], writes=[("ws", s)], dsem=ld_sems[s])
                    self.next_dma += 1

            def take(self, k):
                self.pump()
                aps = []
                res = set()
                for i in range(self.consumed, self.consumed + k):
                    gq = i // G
                    assert gq < self.next_dma, "weight group not scheduled"
                    s = gq % NSLOT
                    o = (i % G) * 128
                    aps.append(ring[:, s, o:o + 128])
                    res.add(("ws", s))
                self.consumed += k
                return aps, sorted(res)

        W = WStream()

        tmp_rr = [0]

        def newtmp():
            i = tmp_rr[0] % NTMP
            tmp_rr[0] += 1
            return i

        def tv(i, lo=0, n=None):
            n = T if n is None else n
            return tmp[:, i, lo:lo + n]

        bank_rr = [0]
        RB = [0, 1, 2, 3, 4]

        def nb(pool=None):
            pool = RB if pool is None else pool
            b = pool[bank_rr[0] % len(pool)]
            bank_rr[0] += 1
            return b

        pending = []

        def mm_group(bank, rhs_list, reads, cols=None):
            n = slice(0, T) if cols is None else cols
            k = len(rhs_list)
            to_flush = pending[:]
            del pending[:]
            i = 0
            while i < k:
                seg = min(k - i, G - (W.consumed % G))
                tiles, sres = W.take(seg)

                def fn(pe, tiles=tiles, rl=rhs_list[i:i + seg], bank=bank, n=n, first=(i == 0), last=(i + seg == k)):
                    inst = None
                    for q, (wt, rhs) in enumerate(zip(tiles, rl)):
                        inst = pe.matmul(out=ps[bank][:, n], lhsT=wt, rhs=rhs, start=(first and q == 0),
                                         stop=(last and q == len(tiles) - 1))
                    return inst
                P.add("pe", fn, reads=list(reads) + sres, writes=[("ps", bank)])
                W.pump()
                i += seg
            for f in to_flush:
                f()

        def mm_inter(banks, rhs_list, rhs_res, cols=None):
            cols = slice(0, T) if cols is None else cols
            to_flush = pending[:]
            del pending[:]
            nk = len(rhs_list)
            for k in range(nk):
                for bank in banks:
                    tiles, sres = W.take(1)
                    P.add("pe", (lambda pe, wt=tiles[0], rhs=rhs_list[k], bank=bank, k=k:
                                 pe.matmul(out=ps[bank][:, cols], lhsT=wt, rhs=rhs, start=(k == 0), stop=(k == nk - 1))),
                          reads=[rhs_res[k]] + sres, writes=[("ps", bank)])
                    W.pump()
            for f in to_flush:
                f()

        SB = 7

        def act_op(out, in_, func, reads, writes, bias=None, scale=None):
            kw = {}
            if bias is not None:
                kw["bias"] = bias
            if scale is not None:
                kw["scale"] = scale
            P.add("act", (lambda e: e.activation(out=out, in_=in_, func=func, **kw)), reads=reads, writes=writes)

        def tt(eng, out, in0, in1, op, reads, writes):
            P.add(eng, (lambda e: e.tensor_tensor(out=out, in0=in0, in1=in1, op=op)), reads=reads, writes=writes)

        def stt(out, in0, scalar, in1, op0, op1, reads, writes):
            P.add("dve", (lambda e: e.scalar_tensor_tensor(out=out, in0=in0, scalar=scalar, in1=in1, op0=op0, op1=op1)),
                  reads=reads, writes=writes)

        def ts(eng, out, in0, s1, op0, reads, writes, s2=None, op1=None):
            if op1 is None:
                P.add(eng, (lambda e: e.tensor_scalar(out=out, in0=in0, scalar1=s1, scalar2=None, op0=op0)), reads=reads, writes=writes)
            else:
                P.add(eng, (lambda e: e.tensor_scalar(out=out, in0=in0, scalar1=s1, scalar2=s2, op0=op0, op1=op1)), reads=reads, writes=writes)

        def cp(eng, out, in_, reads, writes):
            if eng == "act":
                P.add("act", (lambda e: e.copy(out=out, in_=in_)), reads=reads, writes=writes)
            else:
                P.add(eng, (lambda e: e.tensor_copy(out=out, in_=in_)), reads=reads, writes=writes)

        def tvb(i):
            return tmp[:, i, :].bitcast(BF16)[:, 0:T]

        def stat_sq(j):
            ti = newtmp()
            act_op(tvb(ti), xres[:, j, :], AF.Square, reads=[("x", j)], writes=[("t", ti)])
            return ti

        def stat_mm(j, ti, bank=SB, first=None, last=None, nk=None):
            nk = KD if nk is None else nk
            first = (j == 0) if first is None else first
            last = (j == nk - 1) if last is None else last
            P.add("pe", (lambda e: e.matmul(out=ps[bank][:, 0:T], lhsT=ones_bf[:], rhs=tvb(ti), start=first, stop=last)),
                  reads=[("t", ti), "ones"], writes=[("ps", bank)])

        def rstd_inplace(bank, scale):
            if USE_LNEXP:
                act_op(ps[bank][:, 0:T], ps[bank][:, 0:T], AF.Ln, reads=[("ps", bank), "vecs"], writes=[("ps", bank)],
                       bias=vcol("eps"), scale=scale)
                act_op(ps[bank][:, 0:T], ps[bank][:, 0:T], AF.Exp, reads=[("ps", bank)], writes=[("ps", bank)], scale=-0.5)
            else:
                act_op(ps[bank][:, 0:T], ps[bank][:, 0:T], AF.Sqrt, reads=[("ps", bank), "vecs"], writes=[("ps", bank)],
                       bias=vcol("eps"), scale=scale)
                P.add("dve", (lambda e: e.reciprocal(out=ps[bank][:, 0:T], in_=ps[bank][:, 0:T])), reads=[("ps", bank)], writes=[("ps", bank)])

        def norm_apply(gname, dst, dres):
            rstd_inplace(SB, 1.0 / c.D)
            for j in range(KD):
                gc_ = vcol(gname, j)
                stt(dst[:, j, :], xres[:, j, :], gc_, ps[SB][:, 0:T], ALU.mult, ALU.mult,
                    reads=[("x", j), ("ps", SB)], writes=[(dres, j)])

        P.add("sp", (lambda e: e.dma_start(out=vecs[:], in_=vecs_d)), writes=["vecs"], dsem=misc_sem)
        P.add("dve", (lambda e: e.memset(ones_bf[:], 1.0)), writes=["ones"])
        P.add("pool", (lambda e: e.memset(carA[:], 0.0)), writes=[("cA", i) for i in range(c.depth * KA)])
        P.add("pool", (lambda e: e.memset(carB[:], 0.0)), writes=[("cB", i) for i in range(c.depth * KB)])
        P.add("pool", (lambda e: e.memset(carC[:], 0.0)), writes=[("cC", i) for i in range(c.depth * KC)])
        ident = vecs[:, VL.off["ident"]:VL.off["ident"] + 128]

        def blocks(lo, hi):
            out = []
            c0 = lo
            while c0 < hi:
                n = min(128, hi - c0)
                out.append((c0, n))
                c0 += n
            return out

        in_ctr = [0]
        out_ctr = [0]

        for t in range(NT):
            for (c0, nbk) in blocks(0, T):
                si = in_ctr[0] % 2
                in_ctr[0] += 1
                r0 = t * T + c0
                P.add("sp", (lambda e, si=si, r0=r0, nbk=nbk: e.dma_start(out=stg_in[0:nbk, si, :], in_=xin[r0:r0 + nbk, :])),
                      writes=[("si", si)], dsem=xin_sems[si])
                for q in range((KD + 3) // 4):
                    b = nb()
                    js = list(range(4 * q, min(4 * q + 4, KD)))

                    def fn(pe, b=b, js=js, si=si, nbk=nbk):
                        inst = None
                        for jj, j in enumerate(js):
                            inst = pe.transpose(out=ps[b][:, jj * 128:jj * 128 + nbk], in_=stg_in[0:nbk, si, j * 128:(j + 1) * 128],
                                                identity=ident[0:nbk, 0:nbk])
                        return inst
                    P.add("pe", fn, reads=[("si", si), "vecs"], writes=[("ps", b)])
                    nj = len(js)
                    src = ps[b][:, 0:nj * 128].rearrange("p (k t) -> p k t", k=nj)[:, :, 0:nbk]
                    dstv = xres[:, js[0]:js[0] + nj, c0:c0 + nbk]
                    cp("dve", dstv, src, reads=[("ps", b)], writes=[("x", j) for j in js])

            for l in range(c.depth):
                last_layer = (l == c.depth - 1)
                if l == 0:
                    for j in range(KD):
                        ti = stat_sq(j)
                        stat_mm(j, ti)
                norm_apply(("g_mix", l), hb, "h")
                hreads = [("h", j) for j in range(KD)]
                hch = [hb[:, k, :] for k in range(KD)]

                MEANB = SB
                EX2B = 4
                RBm = [0, 1, 2, 3]
                CONVB = [5, 6]
                def conv_group(j):
                    cb = CONVB[j % 2]
                    mm_group(cb, [ua[:, j, k:k + T] for k in range(c.KP)], reads=[("ua", j)])
                    for k in range(c.KP, c.KTAP):
                        stt(ps[cb][:, 0:T], ua[:, j, k:k + T], vcol(("w_dw_a", l), k * KA + j), ps[cb][:, 0:T], ALU.mult, ALU.add,
                            reads=[("ua", j), ("ps", cb), "vecs"], writes=[("ps", cb)])
                    bd = vcol(("b_dw_a", l), j)
                    act_op(vaf[:, j, :], ps[cb][:, 0:T], AF.Identity, reads=[("ps", cb), "vecs"], writes=[("vaf", j)], bias=bd)
                    vb = newtmp()
                    act_op(tvb(vb), ps[cb][:, 0:T], AF.Identity, reads=[("ps", cb), "vecs"], writes=[("t", vb)], bias=bd)
                    sq = newtmp()
                    act_op(tvb(sq), ps[cb][:, 0:T], AF.Square, reads=[("ps", cb), "vecs"], writes=[("t", sq)], bias=bd)

                    def stats(j=j, sq=sq, vb=vb):
                        stat_mm(j, vb, bank=MEANB, nk=KA)
                        stat_mm(j, sq, bank=EX2B, nk=KA)
                    pending.append(stats)

                nil = min(2, KA)
                ilb = [nb(RBm) for _ in range(2 * nil)]
                mm_inter(ilb, hch, hreads)
                for j in range(KA):
                    if j < nil:
                        bv, bg = ilb[2 * j], ilb[2 * j + 1]
                    else:
                        bv = nb(RBm)
                        mm_group(bv, hch, hreads)
                        bg = nb(RBm)
                        mm_group(bg, hch, hreads)
                    ca = l * KA + j
                    cp("pool", ua[:, j, 0:HA], carA[:, ca, :], reads=[("cA", ca), ("h", 0)], writes=[("ua", j)])
                    sg = newtmp()
                    act_op(tv(sg), ps[bg][:, 0:T], AF.Sigmoid, reads=[("ps", bg), "vecs"], writes=[("t", sg)], bias=vcol(("b_gt", l), j))
                    stt(ua[:, j, HA:HA + T], ps[bv][:, 0:T], vcol(("b_val", l), j), tv(sg), ALU.add, ALU.mult,
                        reads=[("ps", bv), ("t", sg)], writes=[("ua", j)])
                    if t == 0:
                        ts("dve", ua[:, j, HA:HA + HALO], ua[:, j, HA:HA + HALO], vcol("mask"), ALU.mult,
                           reads=[("ua", j)], writes=[("ua", j)])
                    cp("pool", carA[:, ca, :], ua[:, j, T:T + HA], reads=[("ua", j)], writes=[("cA", ca)])
                    if j >= 1:
                        conv_group(j - 1)

                for g in range(KB):
                    b = nb(RBm)
                    mm_group(b, hch, hreads)
                    if g == 0:
                        conv_group(KA - 1)
                    cbi = l * KB + g
                    cp("pool", zbx[:, g, 0:HB], carB[:, cbi, :], reads=[("cB", cbi), ("h", 0)], writes=[("zb", g)])
                    cp("act", zbx[:, g, HB:HB + T], ps[b][:, 0:T], reads=[("ps", b)], writes=[("zb", g)])
                    cp("pool", carB[:, cbi, :], zbx[:, g, T:T + HB], reads=[("zb", g)], writes=[("cB", cbi)])
                    d = g + 1
                    w = 2 ** d
                    L0 = HB + T
                    prev = zbx[:, g, :]
                    prev_res = ("zb", g)
                    for i in range(1, d + 1):
                        lo = HB - (w - 2 ** i)
                        sh = 2 ** (i - 1)
                        ti = newtmp()
                        tt("pool", tmp[:, ti, lo:L0], prev[:, lo:L0], prev[:, lo - sh:L0 - sh], ALU.add,
                           reads=[prev_res], writes=[("t", ti)])
                        prev = tmp[:, ti, :]
                        prev_res = ("t", ti)
                    stt(pl[:, g, :], prev[:, HB:HB + T], 1.0 / w, zbx[:, g, HB:HB + T], ALU.mult, ALU.subtract,
                        reads=[prev_res, ("zb", g)], writes=[("pl", g)])
                    if t == 0:
                        t16 = newtmp()
                        rc = vecs[:, VL.off["rc0"] + g * 16:VL.off["rc0"] + (g + 1) * 16]
                        tt("dve", tmp[:, t16, 0:16], prev[:, HB + HALO:HB + HALO + 16], rc, ALU.mult,
                           reads=[prev_res, "vecs"], writes=[("t", t16)])
                        tt("dve", pl[:, g, HALO:HALO + 16], tmp[:, t16, 0:16], zbx[:, g, HB + HALO:HB + HALO + 16], ALU.subtract,
                           reads=[("t", t16), ("zb", g)], writes=[("pl", g)])

                for j in range(KC):
                    bgc = nb(RBm)
                    mm_group(bgc, hch, hreads)
                    bxv = nb(RBm)
                    mm_group(bxv, hch, hreads)
                    bgb = nb(RBm)
                    mm_group(bgb, hch, hreads)
                    cci = l * KC + j
                    cp("pool", vx[:, j, 0:HC], carC[:, cci, :], reads=[("cC", cci), ("h", 0)], writes=[("vx", j)])
                    gcs = newtmp()
                    cp("act", tv(gcs), ps[bgc][:, 0:T], reads=[("ps", bgc)], writes=[("t", gcs)])
                    tt("dve", vx[:, j, HC:HC + T], tv(gcs), ps[bxv][:, 0:T], ALU.mult, reads=[("t", gcs), ("ps", bxv)], writes=[("vx", j)])
                    cp("pool", carC[:, cci, :], vx[:, j, T:T + HC], reads=[("vx", j)], writes=[("cC", cci)])
                    cc = newtmp()
                    ts("dve", tv(cc), vx[:, j, 0:T], vcol(("w_dw_c", l), 0 * KC + j), ALU.mult, reads=[("vx", j), "vecs"], writes=[("t", cc)])
                    for k in (1, 2):
                        stt(tv(cc), vx[:, j, k:k + T], vcol(("w_dw_c", l), k * KC + j), tv(cc), ALU.mult, ALU.add,
                            reads=[("vx", j), ("t", cc)], writes=[("t", cc)])
                    tt("dve", mo[:, KA + KB + j, :], tv(cc), ps[bgb][:, 0:T], ALU.mult, reads=[("t", cc), ("ps", bgb)], writes=[("mo", KA + KB + j)])

                for g in range(KB):
                    b = nb(RBm)
                    mm_group(b, [pl[:, g, :]], reads=[("pl", g)])
                    act_op(mo[:, KA + g, :], ps[b][:, 0:T], AF.Copy, reads=[("ps", b), "vecs"], writes=[("mo", KA + g)],
                           scale=vcol(("pool_scale", l), g))

                m2 = newtmp()
                ida = 1.0 / c.DA
                act_op(tv(m2), ps[MEANB][:, 0:T], AF.Square, reads=[("ps", MEANB)], writes=[("t", m2)], scale=ida)
                stt(ps[EX2B][:, 0:T], ps[EX2B][:, 0:T], ida, tv(m2), ALU.mult, ALU.subtract,
                    reads=[("ps", EX2B), ("t", m2)], writes=[("ps", EX2B)])
                rstd_inplace(EX2B, 1.0)
                for j in range(KA):
                    t1 = newtmp()
                    stt(tv(t1), ps[MEANB][:, 0:T], -ida, vaf[:, j, :], ALU.mult, ALU.add,
                        reads=[("vaf", j), ("ps", MEANB)], writes=[("t", t1)])
                    t2 = newtmp()
                    tt("dve", tv(t2), tv(t1), ps[EX2B][:, 0:T], ALU.mult, reads=[("t", t1), ("ps", EX2B)], writes=[("t", t2)])
                    act_op(mo[:, j, :], tv(t2), AF.Silu, reads=[("t", t2), "vecs"], writes=[("mo", j)],
                           bias=vcol(("ln_b", l), j), scale=vcol(("ln_g", l), j))

                if t == 0:
                    c3 = HALO if last_layer else max(0, (HALO - HA - 2) // 2 * 2)
                else:
                    c3 = 0
                cs = slice(c3, T)
                nT = T - c3
                hcs = [hb[:, k, cs] for k in range(KD)]
                AB = [0, 1, 2, 3, 4, 5, 6, 7]
                ysrc = [
                    ([mo[:, k, cs] for k in range(KA)], [("mo", k) for k in range(KA)]),
                    ([mo[:, KA + k, cs] for k in range(KB)], [("mo", KA + k) for k in range(KB)]),
                    ([mo[:, KA + KB + k, cs] for k in range(KC)], [("mo", KA + KB + k) for k in range(KC)]),
                ]
                for m in range(KD):
                    tis = []
                    ss = []
                    for i in range(3):
                        bg = nb(AB)
                        mm_group(bg, hcs, hreads, cols=cs)
                        s_ = newtmp()
                        act_op(tv(s_, c3, nT), ps[bg][:, cs], AF.Sigmoid, reads=[("ps", bg), "vecs"], writes=[("t", s_)],
                               bias=vcol(("b_gate", l), i * KD + m))
                        ss.append(s_)
                    for i in range(3):
                        by = nb(AB)
                        mm_group(by, ysrc[i][0], ysrc[i][1], cols=cs)
                        ti = newtmp()
                        tt("dve", tv(ti, c3, nT), tv(ss[i], c3, nT), ps[by][:, cs], ALU.mult, reads=[("t", ss[i]), ("ps", by)], writes=[("t", ti)])
                        tis.append(ti)
                    u = newtmp()
                    tt("pool", tv(u, c3, nT), tv(tis[0], c3, nT), tv(tis[1], c3, nT), ALU.add, reads=[("t", tis[0]), ("t", tis[1])], writes=[("t", u)])
                    tt("pool", mg[:, m, cs], tv(u, c3, nT), tv(tis[2], c3, nT), ALU.add, reads=[("t", u), ("t", tis[2])], writes=[("mg", m)])

                RB7 = [0, 1, 2, 3, 4, 5, 6]

                def x_update(ngrp_k, rhs_of_k, rhs_reads, mask_halo):
                    pend = None
                    for n in range(KD):
                        b = nb(RB7)
                        mm_group(b, [rhs_of_k(k) for k in range(ngrp_k)], reads=rhs_reads, cols=cs)
                        if pend is not None:
                            stat_mm(*pend)
                        tt("dve", xres[:, n, cs], xres[:, n, cs], ps[b][:, cs], ALU.add, reads=[("x", n), ("ps", b)], writes=[("x", n)])
                        if mask_halo:
                            ts("dve", xres[:, n, 0:HALO], xres[:, n, 0:HALO], vcol("mask"), ALU.mult, reads=[("x", n), "vecs"], writes=[("x", n)])
                        ti = stat_sq(n)
                        pend = (n, ti)
                    stat_mm(*pend)

                x_update(KD, (lambda k: mg[:, k, cs]), [("mg", k) for k in range(KD)], False)

                norm_apply(("g_mlp", l), hb, "h")
                nup = min(4, KF)
                upb = [nb(RB7) for _ in range(nup)]
                mm_inter(upb, hcs, hreads, cols=cs)
                for j in range(KF):
                    if j < nup:
                        b = upb[j]
                    else:
                        b = nb(RB7)
                        mm_group(b, hcs, hreads, cols=cs)
                    r = newtmp()
                    act_op(tv(r, c3, nT), ps[b][:, cs], AF.Relu, reads=[("ps", b)], writes=[("t", r)])
                    tt("dve", actv[:, j, cs], tv(r, c3, nT), ps[b][:, cs], ALU.mult, reads=[("t", r), ("ps", b)], writes=[("a", j)])
                x_update(KF, (lambda k: actv[:, k, cs]), [("a", k) for k in range(KF)], (t == 0 and not last_layer))

            norm_apply("g_final", ofv, "of")
            lo = HALO if t == 0 else 0
            for (c0, nbk) in blocks(lo, T):
                so = out_ctr[0] % 2
                out_ctr[0] += 1
                for q in range((KD + 3) // 4):
                    b = nb(RB7)
                    js = list(range(4 * q, min(4 * q + 4, KD)))

                    def fn(pe, b=b, js=js, c0=c0, nbk=nbk):
                        inst = None
                        for jj, j in enumerate(js):
                            inst = pe.transpose(out=ps[b][0:nbk, jj * 128:(jj + 1) * 128], in_=ofv[:, j, c0:c0 + nbk], identity=ident)
                        return inst
                    P.add("pe", fn, reads=[("of", j) for j in js] + ["vecs"], writes=[("ps", b)])
                    nj = len(js)
                    cp("act", stg_out[0:nbk, so, js[0] * 128:(js[0] + nj) * 128], ps[b][0:nbk, 0:nj * 128], reads=[("ps", b)], writes=[("so", so)])
                r0 = t * T + c0 - HALO
                P.add("sp", (lambda e, so=so, r0=r0, nbk=nbk: e.dma_start(out=yout[r0:r0 + nbk, :], in_=stg_out[0:nbk, so, :])),
                      reads=[("so", so)], dsem=out_sems[so])
            assert W.consumed == t * ntp + ntiles, (W.consumed, t, ntp, ntiles)
            W.consumed += ntp - ntiles
            W.pump()

        P.finalize(esems)

        with nc.Block() as block:
            @block.tensor
            def _(e):
                P.emit("pe", e)

            @block.scalar
            def _(e):
                P.emit("act", e)

            @block.vector
            def _(e):
                P.emit("dve", e)

            @block.gpsimd
            def _(e):
                P.emit("pool", e)

            @block.sync
            def _(e):
                P.emit("sp", e, final_waits=out_sems + st_sems)
    return nc


def host_prep(c, inp):
    VL = VecLayout(c)
    segs = weight_segments(c)
    tiles_per_layer = sum((len(s[1]) * s[2]) if s[0] == "il" else s[2] for s in segs)
    ntiles = tiles_per_layer * c.depth
    G = c.G
    ngroups = (ntiles + G - 1) // G
    ntp = ngroups * G
    wfa = np.zeros((128, ntp, 128), np.float32)
    i = 0
    for l in range(c.depth):
        mats = {
            "w_in": inp["w_in"][l], "w_gate": inp["w_gate"][l], "w_a_out": inp["w_a_out"][l],
            "w_b_out": inp["w_b_out"][l], "w_c_out": inp["w_c_out"][l], "w_o": inp["w_o"][l],
            "w_up": inp["w_up"][l], "w_down": inp["w_down"][l],
        }
        for (name, cb, kt) in segs:
            if name == "il":
                for k in range(kt):
                    for (nm, cbb) in cb:
                        wfa[:, i, :] = mats[nm][k * 128:(k + 1) * 128, cbb * 128:(cbb + 1) * 128]
                        i += 1
                continue
            if name == "w_pool":
                blk = inp["w_pool_grp"][l][cb]
            elif name == "dwa":
                wd = np.asarray(inp["w_dw_a"][l], np.float32)[:kt, cb * 128:(cb + 1) * 128]
                blk = np.zeros((kt, 128, 128), np.float32)
                ar = np.arange(128)
                blk[:, ar, ar] = wd
            else:
                blk = mats[name][:, cb * 128:(cb + 1) * 128]
            wfa[:, i:i + kt, :] = np.asarray(blk).reshape(kt, 128, 128).transpose(1, 0, 2)
            i += kt
    assert i == ntiles
    wfa = wfa.reshape(128, ntp * 128)

    def cols(v, k):
        return np.asarray(v, np.float32).reshape(k, 128).T

    vbase = np.zeros((128, VL.n), np.float32)

    def put(name, arr):
        o = VL.off[name]
        vbase[:, o:o + arr.shape[1]] = arr

    KD, KA, KB, KC = c.KD, c.KA, c.KB, c.KC
    for l in range(c.depth):
        put(("g_mix", l), cols(inp["g_mix"][l], KD))
        put(("b_val", l), cols(inp["b_glu"][l][:c.DA], KA))
        put(("b_gt", l), cols(inp["b_glu"][l][c.DA:], KA))
        wd = np.asarray(inp["w_dw_a"][l], np.float32)
        put(("w_dw_a", l), wd.reshape(c.KTAP, KA, 128).transpose(2, 0, 1).reshape(128, c.KTAP * KA))
        put(("b_dw_a", l), cols(inp["b_dw_a"][l], KA))
        put(("ln_g", l), cols(inp["ln_g_a"][l], KA))
        put(("ln_b", l), cols(inp["ln_b_a"][l], KA))
        put(("pool_scale", l), cols(inp["pool_scale"][l], KB))
        wc = np.asarray(inp["w_dw_c"][l], np.float32)
        put(("w_dw_c", l), wc.reshape(3, KC, 128).transpose(2, 0, 1).reshape(128, 3 * KC))
        put(("b_gate", l), cols(inp["b_gate"][l], 3 * KD))
        put(("g_mlp", l), cols(inp["g_mlp"][l], KD))
    put("g_final", cols(inp["g_final"], KD))
    put("eps", np.full((128, 1), EPS, np.float32))
    put("ident", np.eye(128, dtype=np.float32))

    x = np.asarray(inp["x"], np.float32)
    in_maps = []
    for core in range(c.ncores):
        b = core // c.cps
        part = core % c.cps
        s0 = part * c.TOK
        xc = np.zeros((c.NT * c.T, c.D), np.float32)
        if part > 0:
            xc[:c.HALO] = x[b, s0 - c.HALO:s0]
        xc[c.HALO:] = x[b, s0:s0 + c.TOK]
        v = vbase.copy()
        v[:, VL.off["mask"]] = 1.0 if part > 0 else 0.0
        rc = np.zeros((4, 16), np.float32)
        for g, w in enumerate(POOL_W):
            for tt_ in range(16):
                rc[g, tt_] = 1.0 / w if part > 0 else 1.0 / min(tt_ + 1, w)
        v[:, VL.off["rc0"]:VL.off["rc0"] + 64] = rc.reshape(1, 64)
        in_maps.append({"xin": xc, "wf": wfa, "vecs": v})
    return in_maps


_CACHE = {}


def run(c, inp, trace=False):
    key = tuple(sorted((k, v) for k, v in c.__dict__.items()))
    if key not in _CACHE:
        _CACHE[key] = build_program(c)
    nc = _CACHE[key]
    in_maps = host_prep(c, inp)
    res = run_bass_kernel_spmd(nc, in_maps, core_ids=list(range(c.ncores)), **({"trace": True} if trace else {}))
    out = np.empty((c.batch, c.seq, c.D), np.float32)
    for core in range(c.ncores):
        b = core // c.cps
        part = core % c.cps
        out[b, part * c.TOK:(part + 1) * c.TOK] = res.results[core]["yout"]
    return out, res


def kernel(**inputs):
    c = Cfg()
    out, _ = run(c, inputs)
    return out
```

```python
import numpy as np
import concourse.bass as bass
import concourse.mybir as mybir
from concourse.bass_utils import run_bass_kernel_spmd

F32 = mybir.dt.float32
BF16 = mybir.dt.bfloat16
AF = mybir.ActivationFunctionType
ALU = mybir.AluOpType
EPS = 1e-6
USE_LNEXP = True
POOL_W = (2, 4, 8, 16)


class Cfg:
    def __init__(self, **kw):
        self.D = 2048
        self.DA = 768
        self.DC = 768
        self.DFF = 8192
        self.KTAP = 31
        self.depth = 2
        self.batch = 4
        self.seq = 4096
        self.ncores = 8
        self.T = 424
        self.NT = 5
        self.HALO = 72
        self.G = 16
        self.NSLOT = 8
        for k, v in kw.items():
            setattr(self, k, v)
        self.KD = self.D // 128
        self.KA = self.DA // 128
        self.KB = 4
        self.KC = self.DC // 128
        self.KF = self.DFF // 128
        self.DB = 512
        self.DIN = 2 * self.DA + self.DB + 3 * self.DC
        self.TOK = self.batch * self.seq // self.ncores
        assert self.NT * self.T - self.HALO == self.TOK
        self.cps = self.ncores // self.batch
        self.HA = self.KTAP - 1
        self.HB = 15
        self.HC = 2


def weight_segments(c):
    KD, KA, KB, KC, KF = c.KD, c.KA, c.KB, c.KC, c.KF
    segs = []
    nil = min(2, KA)
    segs.append(("il", [("w_in", cb) for j in range(nil) for cb in (j, KA + j)], KD))
    for j in range(KA):
        if j >= nil:
            segs.append(("w_in", j, KD))
            segs.append(("w_in", KA + j, KD))
        if j >= 1:
            segs.append(("dwa", j - 1, c.KTAP))
    for g in range(KB):
        segs.append(("w_in", 2 * KA + g, KD))
        if g == 0:
            segs.append(("dwa", KA - 1, c.KTAP))
    for j in range(KC):
        segs.append(("w_in", 2 * KA + KB + KC + j, KD))
        segs.append(("w_in", 2 * KA + KB + 2 * KC + j, KD))
        segs.append(("w_in", 2 * KA + KB + j, KD))
    for g in range(KB):
        segs.append(("w_pool", g, 1))
    for m in range(KD):
        segs.append(("w_gate", m, KD))
        segs.append(("w_gate", KD + m, KD))
        segs.append(("w_gate", 2 * KD + m, KD))
        segs.append(("w_a_out", m, KA))
        segs.append(("w_b_out", m, KB))
        segs.append(("w_c_out", m, KC))
    for n in range(KD):
        segs.append(("w_o", n, KD))
    segs.append(("il", [("w_up", j) for j in range(min(4, KF))], KD))
    for j in range(min(4, KF), KF):
        segs.append(("w_up", j, KD))
    for n in range(KD):
        segs.append(("w_down", n, KF))
    return segs


class VecLayout:
    def __init__(self, c):
        self.off = {}
        self.n = 0
        for l in range(c.depth):
            self.reg(("g_mix", l), c.KD)
            self.reg(("b_val", l), c.KA)
            self.reg(("b_gt", l), c.KA)
            self.reg(("w_dw_a", l), c.KTAP * c.KA)
            self.reg(("b_dw_a", l), c.KA)
            self.reg(("ln_g", l), c.KA)
            self.reg(("ln_b", l), c.KA)
            self.reg(("pool_scale", l), c.KB)
            self.reg(("w_dw_c", l), 3 * c.KC)
            self.reg(("b_gate", l), 3 * c.KD)
            self.reg(("g_mlp", l), c.KD)
        self.reg("g_final", c.KD)
        self.reg("mask", 1)
        self.reg("eps", 1)
        self.reg("rc0", 4 * 16)
        self.reg("ident", 128)

    def reg(self, name, n):
        self.off[name] = self.n
        self.n += n


class _Sem:
    def __init__(self, h):
        self.h = h
        self.count = 0


class _Op:
    __slots__ = ("eng", "fn", "deps", "token", "is_dma", "dsem", "signal", "idx")


class Prog:
    ENGS = ("pe", "act", "dve", "pool", "sp")

    def __init__(self):
        self.ops = {e: [] for e in self.ENGS}
        self.last_writer = {}
        self.readers = {}
        self.nops = 0

    def add(self, eng, fn, reads=(), writes=(), dsem=None):
        op = _Op()
        op.eng = eng
        op.fn = fn
        op.is_dma = dsem is not None
        op.dsem = dsem
        op.signal = False
        op.token = None
        op.idx = self.nops
        self.nops += 1
        deps = {}
        for r in reads:
            w = self.last_writer.get(r)
            if w is not None:
                deps[w.idx] = w
        for wr in writes:
            w = self.last_writer.get(wr)
            if w is not None:
                deps[w.idx] = w
            for rd in self.readers.get(wr, ()):
                deps[rd.idx] = rd
        deps.pop(op.idx, None)
        op.deps = list(deps.values())
        for r in reads:
            self.readers.setdefault(r, []).append(op)
        for wr in writes:
            self.last_writer[wr] = op
            self.readers[wr] = []
        if op.is_dma:
            dsem.count += 16
            op.token = (dsem, dsem.count)
        self.ops[eng].append(op)
        return op

    def finalize(self, esems):
        for e in self.ENGS:
            for op in self.ops[e]:
                for d in op.deps:
                    if d.is_dma:
                        continue
                    if d.eng == "pe" and op.eng == "pe" and not op.is_dma:
                        continue
                    d.signal = True
        for e in self.ENGS:
            s = esems[e]
            for op in self.ops[e]:
                if not op.is_dma and op.signal:
                    s.count += 1
                    op.token = (s, s.count)

    def emit(self, eng_name, eng, final_waits=()):
        waited = {}
        for op in self.ops[eng_name]:
            todo = []
            for d in op.deps:
                if (not d.is_dma) and d.eng == "pe" and eng_name == "pe" and not op.is_dma:
                    continue
                sem, val = d.token
                if waited.get(id(sem), 0) >= val:
                    continue
                waited[id(sem)] = val
                todo.append((d.is_dma, sem, val))
            best = {}
            for (isd, sem, val) in todo:
                if id(sem) not in best or best[id(sem)][2] < val:
                    best[id(sem)] = (isd, sem, val)
            todo = list(best.values())
            pre = None
            if getattr(op.fn, "accepts_pre", False):
                cand = [w for w in todo if not w[0]]
                if cand:
                    w = cand[-1]
                    todo.remove(w)
                    pre = (lambda inst, w=w: inst._wait_ge(w[1].h, w[2]))
            for (isd, sem, val) in todo:
                eng.wait_ge(sem.h, val)
            inst = op.fn(eng, pre) if getattr(op.fn, "accepts_pre", False) else op.fn(eng)
            if op.is_dma:
                inst.then_inc(op.dsem.h, 16)
            elif op.signal:
                inst.then_inc(op.token[0].h, 1)
        for sem in final_waits:
            if sem.count > 0:
                eng.wait_ge(sem.h, sem.count)


def build_program(c):
    T, NT, HALO = c.T, c.NT, c.HALO
    KD, KA, KB, KC, KF = c.KD, c.KA, c.KB, c.KC, c.KF
    HA, HB, HC = c.HA, c.HB, c.HC
    G, NSLOT = c.G, c.NSLOT
    VL = VecLayout(c)
    segs = weight_segments(c)
    tiles_per_layer = sum((len(s[1]) * s[2]) if s[0] == "il" else s[2] for s in segs)
    ntiles = tiles_per_layer * c.depth
    ngroups = (ntiles + G - 1) // G
    ntp = ngroups * G

    nc = bass.Bass("TRN2", target_bir_lowering=False)
    xin = nc.dram_tensor("xin", [NT * T, c.D], F32, kind="ExternalInput").ap()
    wf = nc.dram_tensor("wf", [128, ntp * 128], F32, kind="ExternalInput").ap()
    vecs_d = nc.dram_tensor("vecs", [128, VL.n], F32, kind="ExternalInput").ap()
    wb = nc.dram_tensor("wb", [128, ntp * 128], BF16, kind="Internal").ap()
    yout = nc.dram_tensor("yout", [c.TOK, c.D], F32, kind="ExternalOutput").ap()

    def words(n_elem, dt):
        return (n_elem * (4 if dt == F32 else 2) + 3) // 4
    lay = {}
    cur = 0
    for name, n_elem, dt in (
        ("ua", KA * (HA + T), BF16),
        ("vaf", KA * T, F32),
        ("zb", KB * (HB + T), F32),
        ("vx", KC * (HC + T), F32),
        ("mo", (KA + KB + KC) * T, BF16),
        ("mg", KD * T, BF16),
        ("pl", KB * T, BF16),
    ):
        lay[name] = (cur, words(n_elem, dt))
        cur += words(n_elem, dt)
    scr_words = max(cur, words(KF * T, BF16), words(KD * T, F32))
    NTMP = 12
    TW = HB + T + 1

    import contextlib
    with contextlib.ExitStack() as es:
        def sb(name, shape, dt):
            return es.enter_context(nc.sbuf_tensor(name, shape, dt))

        xres = sb("xres", [128, KD, T], F32)
        hb = sb("hb", [128, KD, T], BF16)
        scr = sb("scr", [128, scr_words], F32)
        tmp = sb("tmp", [128, NTMP, TW], F32)
        ring = sb("ring", [128, NSLOT, G * 128], BF16)
        stg_in = sb("stg_in", [128, 2, c.D], F32)
        stg_out = sb("stg_out", [128, 2, c.D], F32)
        vecs = sb("vecs_sb", [128, VL.n], F32)
        ones_bf = sb("ones_bf", [128, 128], BF16)
        carA = sb("carA", [128, c.depth * KA, HA], BF16)
        carB = sb("carB", [128, c.depth * KB, HB], F32)
        carC = sb("carC", [128, c.depth * KC, HC], F32)
        ps = [es.enter_context(nc.psum_tensor(f"ps{i}", [128, 512], F32)) for i in range(8)]

        def view(name, dt, k):
            o, w = lay[name]
            v = scr[:, o:o + w]
            if dt == BF16:
                v = v.bitcast(BF16)
            n = (w * (1 if dt == F32 else 2)) // k * k
            return v[:, 0:n].rearrange("p (k t) -> p k t", k=k)

        ua = view("ua", BF16, KA)
        vaf = view("vaf", F32, KA)
        zbx = view("zb", F32, KB)
        vx = view("vx", F32, KC)
        mo = view("mo", BF16, KA + KB + KC)
        mg = view("mg", BF16, KD)
        pl = view("pl", BF16, KB)
        actv = scr[:, 0:words(KF * T, BF16)].bitcast(BF16).rearrange("p (k t) -> p k t", k=KF)
        ofv = scr[:, 0:KD * T].rearrange("p (k t) -> p k t", k=KD)

        sem_names = ["pe", "act", "dve", "pool", "sp"]
        esems = {n: _Sem(es.enter_context(nc.semaphore("s_" + n))) for n in sem_names}
        ld_sems = [_Sem(es.enter_context(nc.semaphore(f"ld{i}"))) for i in range(NSLOT)]
        st_sems = [_Sem(es.enter_context(nc.semaphore(f"st{i}"))) for i in range(NSLOT)]
        xin_sems = [_Sem(es.enter_context(nc.semaphore(f"xi{i}"))) for i in range(2)]
        out_sems = [_Sem(es.enter_context(nc.semaphore(f"xo{i}"))) for i in range(2)]
        misc_sem = _Sem(es.enter_context(nc.semaphore("misc")))

        P = Prog()

        def vcol(name, i=0):
            o = VL.off[name] + i
            return vecs[:, o:o + 1]

        class WStream:
            def __init__(self):
                self.consumed = 0
                self.next_dma = 0
                self.total = ntp * NT

            def slot_of_group(self, gq):
                return gq % NSLOT

            def pump(self):
                while self.next_dma * G < self.total and (self.next_dma - NSLOT + 1) * G <= self.consumed:
                    gq = self.next_dma
                    s = gq % NSLOT
                    q = gq % ngroups
                    p = gq // ngroups
                    wp = min(((0,) + (1,) * 2 + (2,) * 4 + (3,) * 8)[q % 15], max(NT - 2, 0))
                    dst = ring[:, s, :]
                    if p <= wp:
                        src = wf[:, q * G * 128:(q + 1) * G * 128]
                        P.add("pool", (lambda e, dst=dst, src=src: e.dma_start(out=dst, in_=src, max_dma_last_dim=8192)),
                              writes=[("ws", s)], dsem=ld_sems[s])
                        if p == wp and NT > 1:
                            dd = wb[:, q * G * 128:(q + 1) * G * 128]
                            P.add("sp", (lambda e, dst=dst, dd=dd: e.dma_start(out=dd, in_=dst)),
                                  reads=[("ws", s)], writes=[("wb", q)], dsem=st_sems[s])
                    else:
                        src = wb[:, q * G * 128:(q + 1) * G * 128]
                        P.add("sp", (lambda e, dst=dst, src=src: e.dma_start(out=dst, in_=src)),
                              reads=[("wb", q)], writes=[("ws", s)], dsem=ld_sems[s])
                    self.next_dma += 1

            def take(self, k):
                self.pump()
                aps = []
                res = set()
                for i in range(self.consumed, self.consumed + k):
                    gq = i // G
                    assert gq < self.next_dma, "weight group not scheduled"
                    s = gq % NSLOT
                    o = (i % G) * 128
                    aps.append(ring[:, s, o:o + 128])
                    res.add(("ws", s))
                self.consumed += k
                return aps, sorted(res)

        W = WStream()

        tmp_rr = [0]

        def newtmp():
            i = tmp_rr[0] % NTMP
            tmp_rr[0] += 1
            return i

        def tv(i, lo=0, n=None):
            n = T if n is None else n
            return tmp[:, i, lo:lo + n]

        bank_rr = [0]
        RB = [0, 1, 2, 3, 4]

        def nb(pool=None):
            pool = RB if pool is None else pool
            b = pool[bank_rr[0] % len(pool)]
            bank_rr[0] += 1
            return b

        pending = []

        def mm_group(bank, rhs_list, reads, cols=None):
            n = slice(0, T) if cols is None else cols
            k = len(rhs_list)
            to_flush = pending[:]
            del pending[:]
            i = 0
            while i < k:
                seg = min(k - i, G - (W.consumed % G))
                tiles, sres = W.take(seg)

                def fn(pe, pre=None, tiles=tiles, rl=rhs_list[i:i + seg], bank=bank, n=n, first=(i == 0), last=(i + seg == k)):
                    inst = None
                    for q, (wt, rhs) in enumerate(zip(tiles, rl)):
                        inst = pe.matmul(out=ps[bank][:, n], lhsT=wt, rhs=rhs, start=(first and q == 0),
                                         stop=(last and q == len(tiles) - 1))
                        if q == 0 and pre is not None:
                            pre(inst)
                    return inst
                fn.accepts_pre = True
                P.add("pe", fn, reads=list(reads) + sres, writes=[("ps", bank)])
                W.pump()
                i += seg
            for f in to_flush:
                f()

        def mm_inter(banks, rhs_list, rhs_res, cols=None):
            cols = slice(0, T) if cols is None else cols
            to_flush = pending[:]
            del pending[:]
            nk = len(rhs_list)
            for k in range(nk):
                for bank in banks:
                    tiles, sres = W.take(1)
                    def fn1(pe, pre=None, wt=tiles[0], rhs=rhs_list[k], bank=bank, k=k):
                        inst = pe.matmul(out=ps[bank][:, cols], lhsT=wt, rhs=rhs, start=(k == 0), stop=(k == nk - 1))
                        if pre is not None:
                            pre(inst)
                        return inst
                    fn1.accepts_pre = True
                    P.add("pe", fn1, reads=[rhs_res[k]] + sres, writes=[("ps", bank)])
                    W.pump()
            for f in to_flush:
                f()

        SB = 7

        def act_op(out, in_, func, reads, writes, bias=None, scale=None):
            kw = {}
            if bias is not None:
                kw["bias"] = bias
            if scale is not None:
                kw["scale"] = scale
            P.add("act", (lambda e: e.activation(out=out, in_=in_, func=func, **kw)), reads=reads, writes=writes)

        def tt(eng, out, in0, in1, op, reads, writes):
            P.add(eng, (lambda e: e.tensor_tensor(out=out, in0=in0, in1=in1, op=op)), reads=reads, writes=writes)

        def stt(out, in0, scalar, in1, op0, op1, reads, writes):
            P.add("dve", (lambda e: e.scalar_tensor_tensor(out=out, in0=in0, scalar=scalar, in1=in1, op0=op0, op1=op1)),
                  reads=reads, writes=writes)

        def ts(eng, out, in0, s1, op0, reads, writes, s2=None, op1=None):
            if op1 is None:
                P.add(eng, (lambda e: e.tensor_scalar(out=out, in0=in0, scalar1=s1, scalar2=None, op0=op0)), reads=reads, writes=writes)
            else:
                P.add(eng, (lambda e: e.tensor_scalar(out=out, in0=in0, scalar1=s1, scalar2=s2, op0=op0, op1=op1)), reads=reads, writes=writes)

        def cp(eng, out, in_, reads, writes):
            if eng == "act":
                P.add("act", (lambda e: e.copy(out=out, in_=in_)), reads=reads, writes=writes)
            else:
                P.add(eng, (lambda e: e.tensor_copy(out=out, in_=in_)), reads=reads, writes=writes)

        def tvb(i):
            return tmp[:, i, :].bitcast(BF16)[:, 0:T]

        def stat_sq(j):
            ti = newtmp()
            act_op(tvb(ti), xres[:, j, :], AF.Square, reads=[("x", j)], writes=[("t", ti)])
            return ti

        def stat_mm(j, ti, bank=SB, first=None, last=None, nk=None):
            nk = KD if nk is None else nk
            first = (j == 0) if first is None else first
            last = (j == nk - 1) if last is None else last
            P.add("pe", (lambda e: e.matmul(out=ps[bank][:, 0:T], lhsT=ones_bf[:], rhs=tvb(ti), start=first, stop=last)),
                  reads=[("t", ti), "ones"], writes=[("ps", bank)])

        def rstd_inplace(bank, scale):
            if USE_LNEXP:
                act_op(ps[bank][:, 0:T], ps[bank][:, 0:T], AF.Ln, reads=[("ps", bank), "vecs"], writes=[("ps", bank)],
                       bias=vcol("eps"), scale=scale)
                act_op(ps[bank][:, 0:T], ps[bank][:, 0:T], AF.Exp, reads=[("ps", bank)], writes=[("ps", bank)], scale=-0.5)
            else:
                act_op(ps[bank][:, 0:T], ps[bank][:, 0:T], AF.Sqrt, reads=[("ps", bank), "vecs"], writes=[("ps", bank)],
                       bias=vcol("eps"), scale=scale)
                P.add("dve", (lambda e: e.reciprocal(out=ps[bank][:, 0:T], in_=ps[bank][:, 0:T])), reads=[("ps", bank)], writes=[("ps", bank)])

        def norm_apply(gname, dst, dres):
            rstd_inplace(SB, 1.0 / c.D)
            for j in range(KD):
                gc_ = vcol(gname, j)
                stt(dst[:, j, :], xres[:, j, :], gc_, ps[SB][:, 0:T], ALU.mult, ALU.mult,
                    reads=[("x", j), ("ps", SB)], writes=[(dres, j)])

        P.add("sp", (lambda e: e.dma_start(out=vecs[:], in_=vecs_d)), writes=["vecs"], dsem=misc_sem)
        P.add("dve", (lambda e: e.memset(ones_bf[:], 1.0)), writes=["ones"])
        P.add("pool", (lambda e: e.memset(carA[:], 0.0)), writes=[("cA", i) for i in range(c.depth * KA)])
        P.add("pool", (lambda e: e.memset(carB[:], 0.0)), writes=[("cB", i) for i in range(c.depth * KB)])
        P.add("pool", (lambda e: e.memset(carC[:], 0.0)), writes=[("cC", i) for i in range(c.depth * KC)])
        ident = vecs[:, VL.off["ident"]:VL.off["ident"] + 128]

        def blocks(lo, hi):
            out = []
            c0 = lo
            while c0 < hi:
                n = min(128, hi - c0)
                out.append((c0, n))
                c0 += n
            return out

        in_ctr = [0]
        out_ctr = [0]

        for t in range(NT):
            for (c0, nbk) in blocks(0, T):
                si = in_ctr[0] % 2
                in_ctr[0] += 1
                r0 = t * T + c0
                P.add("sp", (lambda e, si=si, r0=r0, nbk=nbk: e.dma_start(out=stg_in[0:nbk, si, :], in_=xin[r0:r0 + nbk, :])),
                      writes=[("si", si)], dsem=xin_sems[si])
                for q in range((KD + 3) // 4):
                    b = nb()
                    js = list(range(4 * q, min(4 * q + 4, KD)))

                    def fn(pe, b=b, js=js, si=si, nbk=nbk):
                        inst = None
                        for jj, j in enumerate(js):
                            inst = pe.transpose(out=ps[b][:, jj * 128:jj * 128 + nbk], in_=stg_in[0:nbk, si, j * 128:(j + 1) * 128],
                                                identity=ident[0:nbk, 0:nbk])
                        return inst
                    P.add("pe", fn, reads=[("si", si), "vecs"], writes=[("ps", b)])
                    nj = len(js)
                    src = ps[b][:, 0:nj * 128].rearrange("p (k t) -> p k t", k=nj)[:, :, 0:nbk]
                    dstv = xres[:, js[0]:js[0] + nj, c0:c0 + nbk]
                    cp("dve", dstv, src, reads=[("ps", b)], writes=[("x", j) for j in js])

            for l in range(c.depth):
                last_layer = (l == c.depth - 1)
                if l == 0:
                    for j in range(KD):
                        ti = stat_sq(j)
                        stat_mm(j, ti)
                norm_apply(("g_mix", l), hb, "h")
                hreads = [("h", j) for j in range(KD)]
                hch = [hb[:, k, :] for k in range(KD)]

                MEANB = SB
                EX2B = 4
                RBm = [0, 1, 2, 3]
                CONVB = [5, 6]
                def conv_group(j):
                    cb = CONVB[j % 2]
                    mm_group(cb, [ua[:, j, k:k + T] for k in range(c.KTAP)], reads=[("ua", j)])
                    bd = vcol(("b_dw_a", l), j)
                    act_op(vaf[:, j, :], ps[cb][:, 0:T], AF.Identity, reads=[("ps", cb), "vecs"], writes=[("vaf", j)], bias=bd)
                    vb = newtmp()
                    act_op(tvb(vb), ps[cb][:, 0:T], AF.Identity, reads=[("ps", cb), "vecs"], writes=[("t", vb)], bias=bd)
                    sq = newtmp()
                    act_op(tvb(sq), ps[cb][:, 0:T], AF.Square, reads=[("ps", cb), "vecs"], writes=[("t", sq)], bias=bd)

                    def stats(j=j, sq=sq, vb=vb):
                        stat_mm(j, vb, bank=MEANB, nk=KA)
                        stat_mm(j, sq, bank=EX2B, nk=KA)
                    pending.append(stats)

                nil = min(2, KA)
                ilb = [nb(RBm) for _ in range(2 * nil)]
                mm_inter(ilb, hch, hreads)
                for j in range(KA):
                    if j < nil:
                        bv, bg = ilb[2 * j], ilb[2 * j + 1]
                    else:
                        bv = nb(RBm)
                        mm_group(bv, hch, hreads)
                        bg = nb(RBm)
                        mm_group(bg, hch, hreads)
                    ca = l * KA + j
                    cp("pool", ua[:, j, 0:HA], carA[:, ca, :], reads=[("cA", ca), ("h", 0)], writes=[("ua", j)])
                    sg = newtmp()
                    act_op(tv(sg), ps[bg][:, 0:T], AF.Sigmoid, reads=[("ps", bg), "vecs"], writes=[("t", sg)], bias=vcol(("b_gt", l), j))
                    stt(ua[:, j, HA:HA + T], ps[bv][:, 0:T], vcol(("b_val", l), j), tv(sg), ALU.add, ALU.mult,
                        reads=[("ps", bv), ("t", sg)], writes=[("ua", j)])
                    if t == 0:
                        ts("dve", ua[:, j, HA:HA + HALO], ua[:, j, HA:HA + HALO], vcol("mask"), ALU.mult,
                           reads=[("ua", j)], writes=[("ua", j)])
                    cp("pool", carA[:, ca, :], ua[:, j, T:T + HA], reads=[("ua", j)], writes=[("cA", ca)])
                    if j >= 1:
                        conv_group(j - 1)

                for g in range(KB):
                    b = nb(RBm)
                    mm_group(b, hch, hreads)
                    if g == 0:
                        conv_group(KA - 1)
                    cbi = l * KB + g
                    cp("pool", zbx[:, g, 0:HB], carB[:, cbi, :], reads=[("cB", cbi), ("h", 0)], writes=[("zb", g)])
                    cp("act", zbx[:, g, HB:HB + T], ps[b][:, 0:T], reads=[("ps", b)], writes=[("zb", g)])
                    cp("pool", carB[:, cbi, :], zbx[:, g, T:T + HB], reads=[("zb", g)], writes=[("cB", cbi)])
                    d = g + 1
                    w = 2 ** d
                    L0 = HB + T
                    prev = zbx[:, g, :]
                    prev_res = ("zb", g)
                    for i in range(1, d + 1):
                        lo = HB - (w - 2 ** i)
                        sh = 2 ** (i - 1)
                        ti = newtmp()
                        tt("pool", tmp[:, ti, lo:L0], prev[:, lo:L0], prev[:, lo - sh:L0 - sh], ALU.add,
                           reads=[prev_res], writes=[("t", ti)])
                        prev = tmp[:, ti, :]
                        prev_res = ("t", ti)
                    stt(pl[:, g, :], prev[:, HB:HB + T], 1.0 / w, zbx[:, g, HB:HB + T], ALU.mult, ALU.subtract,
                        reads=[prev_res, ("zb", g)], writes=[("pl", g)])
                    if t == 0:
                        t16 = newtmp()
                        rc = vecs[:, VL.off["rc0"] + g * 16:VL.off["rc0"] + (g + 1) * 16]
                        tt("dve", tmp[:, t16, 0:16], prev[:, HB + HALO:HB + HALO + 16], rc, ALU.mult,
                           reads=[prev_res, "vecs"], writes=[("t", t16)])
                        tt("dve", pl[:, g, HALO:HALO + 16], tmp[:, t16, 0:16], zbx[:, g, HB + HALO:HB + HALO + 16], ALU.subtract,
                           reads=[("t", t16), ("zb", g)], writes=[("pl", g)])

                for j in range(KC):
                    bgc = nb(RBm)
                    mm_group(bgc, hch, hreads)
                    bxv = nb(RBm)
                    mm_group(bxv, hch, hreads)
                    bgb = nb(RBm)
                    mm_group(bgb, hch, hreads)
                    cci = l * KC + j
                    cp("pool", vx[:, j, 0:HC], carC[:, cci, :], reads=[("cC", cci), ("h", 0)], writes=[("vx", j)])
                    gcs = newtmp()
                    cp("act", tv(gcs), ps[bgc][:, 0:T], reads=[("ps", bgc)], writes=[("t", gcs)])
                    tt("dve", vx[:, j, HC:HC + T], tv(gcs), ps[bxv][:, 0:T], ALU.mult, reads=[("t", gcs), ("ps", bxv)], writes=[("vx", j)])
                    cp("pool", carC[:, cci, :], vx[:, j, T:T + HC], reads=[("vx", j)], writes=[("cC", cci)])
                    cc = newtmp()
                    ts("dve", tv(cc), vx[:, j, 0:T], vcol(("w_dw_c", l), 0 * KC + j), ALU.mult, reads=[("vx", j), "vecs"], writes=[("t", cc)])
                    for k in (1, 2):
                        stt(tv(cc), vx[:, j, k:k + T], vcol(("w_dw_c", l), k * KC + j), tv(cc), ALU.mult, ALU.add,
                            reads=[("vx", j), ("t", cc)], writes=[("t", cc)])
                    tt("dve", mo[:, KA + KB + j, :], tv(cc), ps[bgb][:, 0:T], ALU.mult, reads=[("t", cc), ("ps", bgb)], writes=[("mo", KA + KB + j)])

                for g in range(KB):
                    b = nb(RBm)
                    mm_group(b, [pl[:, g, :]], reads=[("pl", g)])
                    act_op(mo[:, KA + g, :], ps[b][:, 0:T], AF.Copy, reads=[("ps", b), "vecs"], writes=[("mo", KA + g)],
                           scale=vcol(("pool_scale", l), g))

                m2 = newtmp()
                ida = 1.0 / c.DA
                act_op(tv(m2), ps[MEANB][:, 0:T], AF.Square, reads=[("ps", MEANB)], writes=[("t", m2)], scale=ida)
                stt(ps[EX2B][:, 0:T], ps[EX2B][:, 0:T], ida, tv(m2), ALU.mult, ALU.subtract,
                    reads=[("ps", EX2B), ("t", m2)], writes=[("ps", EX2B)])
                rstd_inplace(EX2B, 1.0)
                for j in range(KA):
                    t1 = newtmp()
                    stt(tv(t1), ps[MEANB][:, 0:T], -ida, vaf[:, j, :], ALU.mult, ALU.add,
                        reads=[("vaf", j), ("ps", MEANB)], writes=[("t", t1)])
                    t2 = newtmp()
                    tt("dve", tv(t2), tv(t1), ps[EX2B][:, 0:T], ALU.mult, reads=[("t", t1), ("ps", EX2B)], writes=[("t", t2)])
                    act_op(mo[:, j, :], tv(t2), AF.Silu, reads=[("t", t2), "vecs"], writes=[("mo", j)],
                           bias=vcol(("ln_b", l), j), scale=vcol(("ln_g", l), j))

                if t == 0:
                    c3 = HALO if last_layer else max(0, (HALO - HA - 2) // 2 * 2)
                else:
                    c3 = 0
                cs = slice(c3, T)
                nT = T - c3
                hcs = [hb[:, k, cs] for k in range(KD)]
                AB = [0, 1, 2, 3, 4, 5, 6, 7]
                ysrc = [
                    ([mo[:, k, cs] for k in range(KA)], [("mo", k) for k in range(KA)]),
                    ([mo[:, KA + k, cs] for k in range(KB)], [("mo", KA + k) for k in range(KB)]),
                    ([mo[:, KA + KB + k, cs] for k in range(KC)], [("mo", KA + KB + k) for k in range(KC)]),
                ]
                for m in range(KD):
                    tis = []
                    ss = []
                    for i in range(3):
                        bg = nb(AB)
                        mm_group(bg, hcs, hreads, cols=cs)
                        s_ = newtmp()
                        act_op(tv(s_, c3, nT), ps[bg][:, cs], AF.Sigmoid, reads=[("ps", bg), "vecs"], writes=[("t", s_)],
                               bias=vcol(("b_gate", l), i * KD + m))
                        ss.append(s_)
                    for i in range(3):
                        by = nb(AB)
                        mm_group(by, ysrc[i][0], ysrc[i][1], cols=cs)
                        ti = newtmp()
                        tt("dve", tv(ti, c3, nT), tv(ss[i], c3, nT), ps[by][:, cs], ALU.mult, reads=[("t", ss[i]), ("ps", by)], writes=[("t", ti)])
                        tis.append(ti)
                    u = newtmp()
                    tt("pool", tv(u, c3, nT), tv(tis[0], c3, nT), tv(tis[1], c3, nT), ALU.add, reads=[("t", tis[0]), ("t", tis[1])], writes=[("t", u)])
                    tt("pool", mg[:, m, cs], tv(u, c3, nT), tv(tis[2], c3, nT), ALU.add, reads=[("t", u), ("t", tis[2])], writes=[("mg", m)])

                RB7 = [0, 1, 2, 3, 4, 5, 6]

                def x_update(ngrp_k, rhs_of_k, rhs_reads, mask_halo):
                    pend = None
                    for n in range(KD):
                        b = nb(RB7)
                        mm_group(b, [rhs_of_k(k) for k in range(ngrp_k)], reads=rhs_reads, cols=cs)
                        if pend is not None:
                            stat_mm(*pend)
                        tt("dve", xres[:, n, cs], xres[:, n, cs], ps[b][:, cs], ALU.add, reads=[("x", n), ("ps", b)], writes=[("x", n)])
                        if mask_halo:
                            ts("dve", xres[:, n, 0:HALO], xres[:, n, 0:HALO], vcol("mask"), ALU.mult, reads=[("x", n), "vecs"], writes=[("x", n)])
                        ti = stat_sq(n)
                        pend = (n, ti)
                    stat_mm(*pend)

                x_update(KD, (lambda k: mg[:, k, cs]), [("mg", k) for k in range(KD)], False)

                norm_apply(("g_mlp", l), hb, "h")
                nup = min(4, KF)
                upb = [nb(RB7) for _ in range(nup)]
                mm_inter(upb, hcs, hreads, cols=cs)
                for j in range(KF):
                    if j < nup:
                        b = upb[j]
                    else:
                        b = nb(RB7)
                        mm_group(b, hcs, hreads, cols=cs)
                    r = newtmp()
                    act_op(tv(r, c3, nT), ps[b][:, cs], AF.Relu, reads=[("ps", b)], writes=[("t", r)])
                    tt("dve", actv[:, j, cs], tv(r, c3, nT), ps[b][:, cs], ALU.mult, reads=[("t", r), ("ps", b)], writes=[("a", j)])
                x_update(KF, (lambda k: actv[:, k, cs]), [("a", k) for k in range(KF)], (t == 0 and not last_layer))

            norm_apply("g_final", ofv, "of")
            lo = HALO if t == 0 else 0
            for (c0, nbk) in blocks(lo, T):
                so = out_ctr[0] % 2
                out_ctr[0] += 1
                for q in range((KD + 3) // 4):
                    b = nb(RB7)
                    js = list(range(4 * q, min(4 * q + 4, KD)))

                    def fn(pe, b=b, js=js, c0=c0, nbk=nbk):
                        inst = None
                        for jj, j in enumerate(js):
                            inst = pe.transpose(out=ps[b][0:nbk, jj * 128:(jj + 1) * 128], in_=ofv[:, j, c0:c0 + nbk], identity=ident)
                        return inst
                    P.add("pe", fn, reads=[("of", j) for j in js] + ["vecs"], writes=[("ps", b)])
                    nj = len(js)
                    cp("act", stg_out[0:nbk, so, js[0] * 128:(js[0] + nj) * 128], ps[b][0:nbk, 0:nj * 128], reads=[("ps", b)], writes=[("so", so)])
                r0 = t * T + c0 - HALO
                P.add("sp", (lambda e, so=so, r0=r0, nbk=nbk: e.dma_start(out=yout[r0:r0 + nbk, :], in_=stg_out[0:nbk, so, :])),
                      reads=[("so", so)], dsem=out_sems[so])
            assert W.consumed == t * ntp + ntiles, (W.consumed, t, ntp, ntiles)
            W.consumed += ntp - ntiles
            W.pump()

        P.finalize(esems)

        with nc.Block() as block:
            @block.tensor
            def _(e):
                P.emit("pe", e)

            @block.scalar
            def _(e):
                P.emit("act", e)

            @block.vector
            def _(e):
                P.emit("dve", e)

            @block.gpsimd
            def _(e):
                P.emit("pool", e)

            @block.sync
            def _(e):
                P.emit("sp", e, final_waits=out_sems + st_sems)
    return nc


def host_prep(c, inp):
    VL = VecLayout(c)
    segs = weight_segments(c)
    tiles_per_layer = sum((len(s[1]) * s[2]) if s[0] == "il" else s[2] for s in segs)
    ntiles = tiles_per_layer * c.depth
    G = c.G
    ngroups = (ntiles + G - 1) // G
    ntp = ngroups * G
    wfa = np.zeros((128, ntp, 128), np.float32)
    i = 0
    for l in range(c.depth):
        mats = {
            "w_in": inp["w_in"][l], "w_gate": inp["w_gate"][l], "w_a_out": inp["w_a_out"][l],
            "w_b_out": inp["w_b_out"][l], "w_c_out": inp["w_c_out"][l], "w_o": inp["w_o"][l],
            "w_up": inp["w_up"][l], "w_down": inp["w_down"][l],
        }
        for (name, cb, kt) in segs:
            if name == "il":
                for k in range(kt):
                    for (nm, cbb) in cb:
                        wfa[:, i, :] = mats[nm][k * 128:(k + 1) * 128, cbb * 128:(cbb + 1) * 128]
                        i += 1
                continue
            if name == "w_pool":
                blk = inp["w_pool_grp"][l][cb]
            elif name == "dwa":
                wd = np.asarray(inp["w_dw_a"][l], np.float32)[:, cb * 128:(cb + 1) * 128]
                blk = np.zeros((kt, 128, 128), np.float32)
                ar = np.arange(128)
                blk[:, ar, ar] = wd
            else:
                blk = mats[name][:, cb * 128:(cb + 1) * 128]
            wfa[:, i:i + kt, :] = np.asarray(blk).reshape(kt, 128, 128).transpose(1, 0, 2)
            i += kt
    assert i == ntiles
    wfa = wfa.reshape(128, ntp * 128)

    def cols(v, k):
        return np.asarray(v, np.float32).reshape(k, 128).T

    vbase = np.zeros((128, VL.n), np.float32)

    def put(name, arr):
        o = VL.off[name]
        vbase[:, o:o + arr.shape[1]] = arr

    KD, KA, KB, KC = c.KD, c.KA, c.KB, c.KC
    for l in range(c.depth):
        put(("g_mix", l), cols(inp["g_mix"][l], KD))
        put(("b_val", l), cols(inp["b_glu"][l][:c.DA], KA))
        put(("b_gt", l), cols(inp["b_glu"][l][c.DA:], KA))
        wd = np.asarray(inp["w_dw_a"][l], np.float32)
        put(("w_dw_a", l), wd.reshape(c.KTAP, KA, 128).transpose(2, 0, 1).reshape(128, c.KTAP * KA))
        put(("b_dw_a", l), cols(inp["b_dw_a"][l], KA))
        put(("ln_g", l), cols(inp["ln_g_a"][l], KA))
        put(("ln_b", l), cols(inp["ln_b_a"][l], KA))
        put(("pool_scale", l), cols(inp["pool_scale"][l], KB))
        wc = np.asarray(inp["w_dw_c"][l], np.float32)
        put(("w_dw_c", l), wc.reshape(3, KC, 128).transpose(2, 0, 1).reshape(128, 3 * KC))
        put(("b_gate", l), cols(inp["b_gate"][l], 3 * KD))
        put(("g_mlp", l), cols(inp["g_mlp"][l], KD))
    put("g_final", cols(inp["g_final"], KD))
    put("eps", np.full((128, 1), EPS, np.float32))
    put("ident", np.eye(128, dtype=np.float32))

    x = np.asarray(inp["x"], np.float32)
    in_maps = []
    for core in range(c.ncores):
        b = core // c.cps
        part = core % c.cps
        s0 = part * c.TOK
        xc = np.zeros((c.NT * c.T, c.D), np.float32)
        if part > 0:
            xc[:c.HALO] = x[b, s0 - c.HALO:s0]
        xc[c.HALO:] = x[b, s0:s0 + c.TOK]
        v = vbase.copy()
        v[:, VL.off["mask"]] = 1.0 if part > 0 else 0.0
        rc = np.zeros((4, 16), np.float32)
        for g, w in enumerate(POOL_W):
            for tt_ in range(16):
                rc[g, tt_] = 1.0 / w if part > 0 else 1.0 / min(tt_ + 1, w)
        v[:, VL.off["rc0"]:VL.off["rc0"] + 64] = rc.reshape(1, 64)
        in_maps.append({"xin": xc, "wf": wfa, "vecs": v})
    return in_maps


_CACHE = {}


def run(c, inp, trace=False):
    key = tuple(sorted((k, v) for k, v in c.__dict__.items()))
    if key not in _CACHE:
        _CACHE[key] = build_program(c)
    nc = _CACHE[key]
    in_maps = host_prep(c, inp)
    res = run_bass_kernel_spmd(nc, in_maps, core_ids=list(range(c.ncores)), **({"trace": True} if trace else {}))
    out = np.empty((c.batch, c.seq, c.D), np.float32)
    for core in range(c.ncores):
        b = core // c.cps
        part = core % c.cps
        out[b, part * c.TOK:(part + 1) * c.TOK] = res.results[core]["yout"]
    return out, res


def kernel(**inputs):
    c = Cfg()
    out, _ = run(c, inputs)
    return out
```
